# Optimizing a Trainium2 kernel written in Bass

```python
import math
import jax, jax.numpy as jnp
from jax import lax
import numpy as np

D_MODEL = 2048
BATCH = 1
SEQ = 16384
DEPTH = 2

PLE_DIM = 256
D_FF = 5632
DA_HEADS = 8
DA_HEAD_DIM = 64
DA_V_DIM = 2 * DA_HEAD_DIM
RET_HEADS = 8
RET_K_DIM = 128
RET_V_DIM = 128
Q_BLOCK = 128
RET_CHUNK = 128
ROPE_BASE = 10000.0
EPS = 1e-6

DA_QK_W = DA_HEADS * 2 * DA_HEAD_DIM
DA_V_W = DA_HEADS * DA_V_DIM
RET_QK_W = RET_HEADS * RET_K_DIM
RET_V_W = RET_HEADS * RET_V_DIM
IN_SIZES = (DA_QK_W, DA_QK_W, DA_V_W, RET_QK_W, RET_QK_W, RET_V_W, RET_V_W, D_MODEL, D_MODEL)
N_IN = sum(IN_SIZES)

kernel_name = "hybrid_diffattn_retention_gated_macaron"


def rmsnorm(x, g):
    xf = x.astype(jnp.float32)
    y = xf * lax.rsqrt(jnp.mean(xf * xf, axis=-1, keepdims=True) + EPS)
    return (y * g.astype(jnp.float32)).astype(x.dtype)


def swiglu(h, w_gate, w_up, w_down):
    return (jax.nn.silu(h @ w_gate) * (h @ w_up)) @ w_down


def split_in(z):
    offsets = np.cumsum(np.array(IN_SIZES))[:-1].tolist()
    return jnp.split(z, offsets, axis=-1)


def diff_attention(qa, ka, va, q_g, k_g, lq1, lk1, lq2, lk2, sub_g, lambda_init):
    b, s, _ = qa.shape
    q = rmsnorm(qa.reshape(b, s, DA_HEADS, 2, DA_HEAD_DIM), q_g)
    k = rmsnorm(ka.reshape(b, s, DA_HEADS, 2, DA_HEAD_DIM), k_g)
    q = q.transpose(0, 2, 3, 1, 4)
    k = k.transpose(0, 2, 3, 1, 4)
    v = va.reshape(b, s, DA_HEADS, DA_V_DIM).transpose(0, 2, 1, 3)
    lam = (jnp.exp(jnp.sum(lq1.astype(jnp.float32) * lk1.astype(jnp.float32)))
           - jnp.exp(jnp.sum(lq2.astype(jnp.float32) * lk2.astype(jnp.float32)))
           + lambda_init)
    scale = DA_HEAD_DIM ** -0.5
    n_blk = s // Q_BLOCK
    q_blocks = q.reshape(b, DA_HEADS, 2, n_blk, Q_BLOCK, DA_HEAD_DIM).transpose(3, 0, 1, 2, 4, 5)
    key_pos = jnp.arange(s)

    def one_block(args):
        qb, i = args
        sc = jnp.einsum('bhcqd,bhckd->bhcqk', qb, k).astype(jnp.float32) * scale
        q_pos = i * Q_BLOCK + jnp.arange(Q_BLOCK)
        mask = key_pos[None, :] <= q_pos[:, None]
        sc = jnp.where(mask, sc, jnp.finfo(jnp.float32).min)
        a = jax.nn.softmax(sc, axis=-1)
        attn = a[:, :, 0] - lam * a[:, :, 1]
        return jnp.einsum('bhqk,bhkd->bhqd', attn.astype(v.dtype), v)

    o = lax.map(one_block, (q_blocks, jnp.arange(n_blk)))
    o = o.transpose(1, 0, 3, 2, 4).reshape(b, s, DA_HEADS, DA_V_DIM)
    o = rmsnorm(o, sub_g) * (1.0 - lambda_init)
    return o.reshape(b, s, DA_V_W)


def rotary(x, positions):
    d = x.shape[-1]
    inv_freq = ROPE_BASE ** (-jnp.arange(0, d, 2, dtype=jnp.float32) / d)
    ang = positions.astype(jnp.float32)[..., None] * inv_freq
    cos = jnp.cos(ang)[:, :, None, :]
    sin = jnp.sin(ang)[:, :, None, :]
    x1, x2 = x[..., : d // 2], x[..., d // 2:]
    return jnp.concatenate([x1 * cos - x2 * sin, x1 * sin + x2 * cos], axis=-1)


def retention(qr, kr, vr, gr, positions, sub_g):
    b, s, _ = qr.shape
    n, c = s // RET_CHUNK, RET_CHUNK
    q = rotary(qr.astype(jnp.float32).reshape(b, s, RET_HEADS, RET_K_DIM), positions)
    k = rotary(kr.astype(jnp.float32).reshape(b, s, RET_HEADS, RET_K_DIM), positions) * (RET_K_DIM ** -0.5)
    v = vr.astype(jnp.float32).reshape(b, s, RET_HEADS, RET_V_DIM)
    to_chunks = lambda t: t.reshape(b, n, c, RET_HEADS, t.shape[-1]).transpose(0, 3, 1, 2, 4)
    q, k, v = to_chunks(q), to_chunks(k), to_chunks(v)

    log_g = jnp.log(1.0 - 2.0 ** (-5.0 - jnp.arange(RET_HEADS, dtype=jnp.float32)))
    idx = jnp.arange(c, dtype=jnp.float32)
    rel = idx[:, None] - idx[None, :]
    decay_mat = jnp.exp(log_g[:, None, None] * jnp.maximum(rel, 0.0)) * (rel >= 0)
    zeta = jnp.exp(log_g[:, None] * (c - 1 - idx))
    xi = jnp.exp(log_g[:, None] * (idx + 1.0))
    chunk_decay = jnp.exp(log_g * c)

    scores = jnp.einsum('bhncd,bhnmd->bhncm', q, k) * decay_mat[None, :, None]
    o_inner = jnp.einsum('bhncm,bhnme->bhnce', scores, v)
    kv = jnp.einsum('bhnmd,bhnme->bhnde', k * zeta[None, :, None, :, None], v)

    def step(state, kv_n):
        return chunk_decay[None, :, None, None] * state + kv_n, state

    init = jnp.zeros((b, RET_HEADS, RET_K_DIM, RET_V_DIM), jnp.float32)
    _, r_prev = lax.scan(step, init, kv.transpose(2, 0, 1, 3, 4))
    o_cross = jnp.einsum('bhncd,nbhde->bhnce', q * xi[None, :, None, :, None], r_prev)
    o = (o_inner + o_cross).transpose(0, 2, 3, 1, 4).reshape(b, s, RET_HEADS, RET_V_DIM)
    o = rmsnorm(o, sub_g).reshape(b, s, RET_V_W)
    return o * jax.nn.silu(gr.astype(jnp.float32))


def setup_inputs(seed: int = 0) -> dict:
    key = jax.random.key(seed)
    ks = iter(jax.random.split(key, 40))
    L, D, F = DEPTH, D_MODEL, D_FF
    w = lambda shape, fan_in: jax.random.normal(next(ks), shape, jnp.float32) * (fan_in ** -0.5)
    gain = lambda shape: 1.0 + 0.02 * jax.random.normal(next(ks), shape, jnp.float32)
    small = lambda shape: 0.1 * jax.random.normal(next(ks), shape, jnp.float32)
    return {
        "x": jax.random.normal(next(ks), (BATCH, SEQ, D), jnp.float32),
        "p": jax.random.normal(next(ks), (L, BATCH, SEQ, PLE_DIM), jnp.float32),
        "positions": jnp.broadcast_to(jnp.arange(SEQ, dtype=jnp.int32)[None, :], (BATCH, SEQ)),
        "ffn1_norm": gain((L, D)),
        "ffn1_w_gate": w((L, D, F), D),
        "ffn1_w_up": w((L, D, F), D),
        "ffn1_w_down": w((L, F, D), F),
        "mix_norm": gain((L, D)),
        "w_in": w((L, D, N_IN), D),
        "da_q_norm": gain((L, DA_HEAD_DIM)),
        "da_k_norm": gain((L, DA_HEAD_DIM)),
        "da_lambda_q1": small((L, DA_HEAD_DIM)),
        "da_lambda_k1": small((L, DA_HEAD_DIM)),
        "da_lambda_q2": small((L, DA_HEAD_DIM)),
        "da_lambda_k2": small((L, DA_HEAD_DIM)),
        "da_sub_norm": gain((L, DA_V_DIM)),
        "ret_sub_norm": gain((L, RET_V_DIM)),
        "w_up_a": w((L, DA_V_W, D), DA_V_W),
        "w_up_b": w((L, RET_V_W, D), RET_V_W),
        "w_out": w((L, D, D), D),
        "ffn2_norm": gain((L, D)),
        "ffn2_w_gate": w((L, D, F), D),
        "ffn2_w_up": w((L, D, F), D),
        "ffn2_w_down": w((L, F, D), F),
        "ple_norm": gain((L, D)),
        "w_ple_gate": w((L, D, D), D),
        "w_ple_proj": w((L, PLE_DIM, D), PLE_DIM),
    }


def reference(x, p, positions, ffn1_norm, ffn1_w_gate, ffn1_w_up, ffn1_w_down, mix_norm, w_in,
              da_q_norm, da_k_norm, da_lambda_q1, da_lambda_k1, da_lambda_q2, da_lambda_k2,
              da_sub_norm, ret_sub_norm, w_up_a, w_up_b, w_out, ffn2_norm, ffn2_w_gate,
              ffn2_w_up, ffn2_w_down, ple_norm, w_ple_gate, w_ple_proj):
    for i in range(DEPTH):
        lambda_init = 0.8 - 0.6 * math.exp(-0.3 * i)
        x = x + 0.5 * swiglu(rmsnorm(x, ffn1_norm[i]), ffn1_w_gate[i], ffn1_w_up[i], ffn1_w_down[i])
        h = rmsnorm(x, mix_norm[i])
        qa, ka, va, qr, kr, vr, gr, ga, gb = split_in(h @ w_in[i])
        ya = diff_attention(qa, ka, va, da_q_norm[i], da_k_norm[i], da_lambda_q1[i], da_lambda_k1[i],
                            da_lambda_q2[i], da_lambda_k2[i], da_sub_norm[i], lambda_init)
        yb = retention(qr, kr, vr, gr, positions, ret_sub_norm[i]).astype(x.dtype)
        merged = jax.nn.sigmoid(ga) * (ya @ w_up_a[i]) + jax.nn.sigmoid(gb) * (yb @ w_up_b[i])
        x = x + merged @ w_out[i]
        x = x + 0.5 * swiglu(rmsnorm(x, ffn2_norm[i]), ffn2_w_gate[i], ffn2_w_up[i], ffn2_w_down[i])
        gate = jax.nn.sigmoid(rmsnorm(x, ple_norm[i]) @ w_ple_gate[i])
        x = x + gate * (p[i] @ w_ple_proj[i])
    return x
```

```python
import math
from contextlib import ExitStack
import numpy as np
import ml_dtypes
import concourse.bass as bass
import concourse.mybir as mybir
from concourse.bass_utils import run_bass_kernel_spmd

F32 = mybir.dt.float32
BF16 = mybir.dt.bfloat16
I32 = mybir.dt.int32
AF = mybir.ActivationFunctionType
ALU = mybir.AluOpType
AX = mybir.AxisListType

NCORE = 8
EPS = 1e-6
SLOT_ELEMS = 8192
NSLOT = 3
TWO_PI = 2.0 * math.pi
C1 = 6.28125
C2 = TWO_PI - C1

SEG = {"qa": 0, "ka": 1024, "va": 2048, "qr": 3072, "kr": 4096, "vr": 5120, "gr": 6144, "ga": 7168}


class Buf:
    __slots__ = ("name", "lw", "rd_eng", "rd_dma")

    def __init__(self, name):
        self.name = name
        self.lw = None
        self.rd_eng = {}
        self.rd_dma = []


class Op:
    __slots__ = ("eng", "fn", "deps", "dma", "key", "n", "mark", "cnt", "dval")

    def __init__(self, eng, fn, deps, dma, key, n):
        self.eng, self.fn, self.deps, self.dma, self.key, self.n = eng, fn, deps, dma, key, n
        self.mark = False
        self.cnt = None
        self.dval = None


class Prog:
    ENGS = ("pe", "act", "dve", "pool", "sp")

    def __init__(self, nc):
        self.nc = nc
        self.ops = []
        self.bufs = {}
        self.barrier_deps = None
        self.seen_after_barrier = set()

    def buf(self, *key):
        b = self.bufs.get(key)
        if b is None:
            b = self.bufs[key] = Buf(key)
        return b

    def add(self, eng, fn, reads=(), writes=(), dma=False, key=None, n=1):
        idx = len(self.ops)
        deps = set()
        for b in reads:
            if b.lw is not None:
                deps.add(b.lw)
        for b in writes:
            if b.lw is not None:
                deps.add(b.lw)
            deps.update(b.rd_eng.values())
            deps.update(b.rd_dma)
        if self.barrier_deps is not None and eng not in self.seen_after_barrier:
            deps.update(self.barrier_deps)
            self.seen_after_barrier.add(eng)
        op = Op(eng, fn, deps, dma, key, n)
        for b in reads:
            if dma:
                b.rd_dma.append(idx)
            else:
                b.rd_eng[eng] = idx
        for b in writes:
            b.lw = idx
            b.rd_eng = {}
            b.rd_dma = []
        self.ops.append(op)
        return idx

    def barrier(self):
        last = {}
        dmas = []
        for i, op in enumerate(self.ops):
            if op.dma:
                dmas.append(i)
            else:
                last[op.eng] = i
        self.barrier_deps = set(last.values()) | set(dmas)
        self.seen_after_barrier = set()

    def emit(self, es):
        nc = self.nc
        ops = self.ops
        for op in ops:
            for d in op.deps:
                p = ops[d]
                if p.dma:
                    continue
                if p.eng == "pe" and op.eng == "pe" and not op.dma:
                    continue
                p.mark = True
        cnt = {e: 0 for e in self.ENGS}
        dval = {}
        for op in ops:
            if op.dma:
                dval[op.key] = dval.get(op.key, 0) + 16 * op.n
                op.dval = dval[op.key]
            elif op.mark:
                cnt[op.eng] += 1
                op.cnt = cnt[op.eng]
        esem = {e: es.enter_context(nc.semaphore("e_" + e)) for e in self.ENGS}
        dsem = {k: es.enter_context(nc.semaphore("d%d" % i)) for i, k in enumerate(dval)}
        by_eng = {e: [] for e in self.ENGS}
        for op in ops:
            by_eng[op.eng].append(op)

        def run(engname, e):
            known = {}
            for op in by_eng[engname]:
                need = {}
                for d in op.deps:
                    p = ops[d]
                    if p.dma:
                        s, v = dsem[p.key], p.dval
                    else:
                        if p.eng == "pe" and engname == "pe" and not op.dma:
                            continue
                        s, v = esem[p.eng], p.cnt
                    if need.get(s, (None, 0))[1] < v:
                        need[s] = (s, v)
                for s, v in need.values():
                    if known.get(s, 0) < v:
                        e.wait_ge(s, v)
                        known[s] = v
                ins = op.fn(e)
                if op.dma:
                    assert len(ins) == op.n
                    for i in ins:
                        i.then_inc(dsem[op.key], 16)
                elif op.mark:
                    ins.then_inc(esem[engname], 1)
            if engname == "sp":
                for k, v in dval.items():
                    e.wait_ge(dsem[k], v)

        block = es.enter_context(nc.Block())

        @block.tensor
        def _(e):
            run("pe", e)

        @block.scalar
        def _(e):
            run("act", e)

        @block.vector
        def _(e):
            run("dve", e)

        @block.gpsimd
        def _(e):
            run("pool", e)

        @block.sync
        def _(e):
            run("sp", e)


class Rot:
    def __init__(self, P, es, nc, name, n, shape, dt):
        self.items = []
        for i in range(n):
            t = es.enter_context(nc.sbuf_tensor("r_%s%d" % (name, i), shape, dt))
            self.items.append((t, P.buf(name, i)))
        self.i = 0

    def get(self):
        r = self.items[self.i % len(self.items)]
        self.i += 1
        return r


def halves(T):
    return [(lo, min(lo + 512, T)) for lo in range(0, T, 512)]


def wview(slot, off, KC, NC):
    return slot[:, off:off + KC * NC].rearrange("p (k n) -> p k n", n=NC)


class Step:
    def __init__(self, loads, body):
        self.loads = loads
        self.body = body


def run_steps(P, steps, slots):
    widx = [i for i, s in enumerate(steps) if s.loads]
    slot_of = {i: n % len(slots) for n, i in enumerate(widx)}
    state = {"issued": 0}

    def issue_upto(n):
        while state["issued"] < len(widx) and state["issued"] <= n:
            i = widx[state["issued"]]
            s = steps[i]
            k = slot_of[i]
            ap, buf = slots[k]

            def fn(e, s=s, ap=ap):
                return [e.dma_start(out=ap[:, off:off + n], in_=src) for (off, n, src) in s.loads]

            P.add("pool", fn, writes=(buf,), dma=True, key="w%d" % k, n=len(s.loads))
            state["issued"] += 1

    wn = 0
    issue_upto(len(slots) - 2)
    for i, s in enumerate(steps):
        if s.loads:
            issue_upto(wn + len(slots) - 1)
            ap, buf = slots[slot_of[i]]
            s.body(ap, buf)
            wn += 1
        else:
            s.body(None, None)


class TokCtx:
    def __init__(self, nc, P, es, cfg):
        self.nc, self.P, self.es, self.cfg = nc, P, es, cfg
        D, F, T = cfg["D"], cfg["F"], cfg["T"]
        self.DC, self.FC = D // 128, F // 128
        self.FH = self.FC // 2
        DC = self.DC
        sb = lambda name, shape, dt: es.enter_context(nc.sbuf_tensor("s_" + name, shape, dt))
        self.hT = sb("hT", [128, DC, T], BF16)
        self.hTb = [P.buf("hT", i) for i in range(DC)]
        self.act = sb("act", [128, max(self.FH, 16), T], BF16)
        self.actb = [P.buf("act", i) for i in range(max(self.FH, 16))]
        self.slots = [(sb("wslot%d" % i, [128, SLOT_ELEMS], BF16), P.buf("wslot", i)) for i in range(NSLOT)]
        self.p32 = Rot(P, es, nc, "p32_", 8, [128, T], F32)
        self.p16 = Rot(P, es, nc, "p16_", 4, [128, T], BF16)
        self.rstd = sb("rstd", [128, T], F32)
        self.rstdb = P.buf("rstd")
        self.tabs = [(sb("tab%d" % i, [128, T], F32), P.buf("tab", i)) for i in range(4)]
        self.pT = sb("pT", [128, 2, T], BF16)
        self.pTb = P.buf("pT")
        self.ps = [(es.enter_context(nc.psum_tensor("ps%d" % i, [128, 1024], F32)), P.buf("ps", i)) for i in range(4)]
        self.c32 = sb("c32", [128, 264], F32)
        self.c32b = P.buf("c32")
        self.gains = sb("gains", [128, cfg["L"] * (4 * DC + 4)], F32)
        self.gainsb = P.buf("gains")
        self.gsub = sb("gsub", [128, 2 * cfg["L"]], F32)
        self.gsubb = P.buf("gsub")
        self.psi = 0

    def load_consts(self, c32_d, gains_d):
        P = self.P
        P.add("sp", lambda e: [e.dma_start(out=self.c32[:], in_=c32_d)], writes=(self.c32b,), dma=True, key="c32")
        P.add("sp", lambda e: [e.dma_start(out=self.gains[:], in_=gains_d)], writes=(self.gainsb,), dma=True, key="gains")

    @property
    def ones(self):
        return self.c32[:, 0:128]

    @property
    def bd64(self):
        return self.c32[:, 128:256]

    def col(self, i):
        return self.c32[:, 256 + i:257 + i]

    def gcol(self, l, which, dc):
        DC = self.DC
        o = l * (4 * DC + 4) + which * DC + dc
        return self.gains[:, o:o + 1]

    def gcol2(self, l, j):
        DC = self.DC
        o = l * (4 * DC + 4) + 4 * DC + j
        return self.gains[:, o:o + 1]

    def nextps(self):
        r = self.ps[self.psi % 4]
        self.psi += 1
        return r


def rstd_from_ss(C, ss, ssb, n_feat, T):
    P = C.P
    rt, rtb = C.p32.get()
    P.add("act", lambda e: e.activation(out=rt[:, :T], in_=ss[:, :T], func=AF.Sqrt, bias=C.col(2), scale=1.0 / n_feat),
          reads=(ssb, C.c32b), writes=(rtb,))
    P.add("dve", lambda e: e.reciprocal(out=C.rstd[:, :T], in_=rt[:, :T]), reads=(rtb,), writes=(C.rstdb,))


def phase_norm(C, xd, xdb, l, which, t):
    P, T, DC = C.P, C.cfg["T"], C.DC
    tok = slice(t * T, (t + 1) * T)

    def body(_a, _b):
        ss, ssb = C.nextps()
        for dc in range(DC):
            xs, xsb = C.p32.get()
            P.add("sp", lambda e, dc=dc, xs=xs: [e.dma_start(out=xs[:, :T], in_=xd[dc, :, tok])],
                  reads=(xdb(dc, t),), writes=(xsb,), dma=True, key=xsb.name)
            sq, sqb = C.p32.get()
            P.add("act", lambda e, xs=xs, sq=sq: e.activation(out=sq[:, :T], in_=xs[:, :T], func=AF.Square),
                  reads=(xsb,), writes=(sqb,))
            for (lo, hi) in halves(T):
                P.add("pe", lambda e, sq=sq, lo=lo, hi=hi, dc=dc: e.matmul(ss[:, lo:hi], lhsT=C.ones, rhs=sq[:, lo:hi],
                                                                           start=(dc == 0), stop=(dc == DC - 1)),
                      reads=(sqb, C.c32b), writes=(ssb,))
        rstd_from_ss(C, ss, ssb, DC * 128, T)
        for dc in range(DC):
            xs, xsb = C.p32.get()
            P.add("sp", lambda e, dc=dc, xs=xs: [e.dma_start(out=xs[:, :T], in_=xd[dc, :, tok])],
                  reads=(xdb(dc, t),), writes=(xsb,), dma=True, key=xsb.name)
            P.add("dve", lambda e, dc=dc, xs=xs: e.scalar_tensor_tensor(out=C.hT[:, dc, :], in0=xs[:, :T], scalar=C.gcol(l, which, dc),
                                                                         in1=C.rstd[:, :T], op0=ALU.mult, op1=ALU.mult),
                  reads=(xsb, C.rstdb, C.gainsb), writes=(C.hTb[dc],))

    return [Step(None, body)]


def mm_group(C, ps, psb, w, wbuf, col0, rhs3, rhs_bufs, KC, T, first=True, last=True):
    P = C.P
    for kc in range(KC):
        for (lo, hi) in halves(T):
            P.add("pe", lambda e, kc=kc, lo=lo, hi=hi: e.matmul(ps[:, lo:hi], lhsT=w[:, kc, col0:col0 + 128], rhs=rhs3[:, kc, lo:hi],
                                                                start=(first and kc == 0), stop=(last and kc == KC - 1)),
                  reads=(wbuf, rhs_bufs[kc]), writes=(psb,))


def residual_store(C, xd, xdb, oc, t, ps, psb, scale, extra=None):
    P, T = C.P, C.cfg["T"]
    tok = slice(t * T, (t + 1) * T)
    xs, xsb = C.p32.get()
    P.add("sp", lambda e: [e.dma_start(out=xs[:, :T], in_=xd[oc, :, tok])], reads=(xdb(oc, t),), writes=(xsb,), dma=True, key=xsb.name)
    xo, xob = C.p32.get()
    src, srcb = (ps, psb) if extra is None else extra
    P.add("dve", lambda e: e.scalar_tensor_tensor(out=xo[:, :T], in0=src[:, :T], scalar=float(scale), in1=xs[:, :T],
                                                  op0=ALU.mult, op1=ALU.add),
          reads=(srcb, xsb), writes=(xob,))
    P.add("sp", lambda e: [e.dma_start(out=xd[oc, :, tok], in_=xo[:, :T])], reads=(xob,), writes=(xdb(oc, t),), dma=True, key=xob.name)


def phase_ffn(C, xd, xdb, wg, wu, wd, t):
    P, T, DC, FH = C.P, C.cfg["T"], C.DC, C.FH
    steps = []
    NCOL = 256
    for half in range(2):
        f0 = half * FH
        for fg in range(FH * 128 // NCOL):
            c0 = f0 * 128 + fg * NCOL
            loads = [(0, DC * NCOL, wg[c0 // NCOL]), (DC * NCOL, DC * NCOL, wu[c0 // NCOL])]

            def body(slot, sbuf, fg=fg):
                g = wview(slot, 0, DC, NCOL)
                u = wview(slot, DC * NCOL, DC, NCOL)
                for j in range(NCOL // 128):
                    fcl = fg * (NCOL // 128) + j
                    gp, gpb = C.nextps()
                    up, upb = C.nextps()
                    mm_group(C, gp, gpb, g, sbuf, j * 128, C.hT, C.hTb, DC, T)
                    mm_group(C, up, upb, u, sbuf, j * 128, C.hT, C.hTb, DC, T)
                    sg, sgb = C.p32.get()
                    P.add("act", lambda e, gp=gp, sg=sg: e.activation(out=sg[:, :T], in_=gp[:, :T], func=AF.Silu),
                          reads=(gpb,), writes=(sgb,))
                    P.add("dve", lambda e, up=up, sg=sg, fcl=fcl: e.tensor_tensor(out=C.act[:, fcl, :], in0=sg[:, :T], in1=up[:, :T], op=ALU.mult),
                          reads=(sgb, upb), writes=(C.actb[fcl],))

            steps.append(Step(loads, body))
        for oc in range(DC):
            loads = [(0, FH * 128, wd[half, oc])]

            def body(slot, sbuf, oc=oc):
                d = wview(slot, 0, FH, 128)
                yp, ypb = C.nextps()
                mm_group(C, yp, ypb, d, sbuf, 0, C.act, C.actb, FH, T)
                residual_store(C, xd, xdb, oc, t, yp, ypb, 0.5)

            steps.append(Step(loads, body))
    return steps


def rotary_tables(C, posd, t):
    P, T = C.P, C.cfg["T"]
    tok = slice(t * T, (t + 1) * T)
    s_k = 128.0 ** -0.5

    def body(_a, _b):
        pi_, pib = C.p32.get()
        pi_i = pi_[:, :].bitcast(I32)
        P.add("sp", lambda e: [e.dma_start(out=pi_i[:, :T], in_=posd[0:1, tok].partition_broadcast(128))], writes=(pib,), dma=True, key=pib.name)
        ang, angb = C.p32.get()
        P.add("dve", lambda e: e.tensor_copy(out=ang[:, :T], in_=pi_i[:, :T]), reads=(pib,), writes=(angb,))
        P.add("dve", lambda e: e.tensor_scalar(out=ang[:, :T], in0=ang[:, :T], scalar1=C.col(0), scalar2=None, op0=ALU.mult),
              reads=(angb, C.c32b), writes=(angb,))

        def reduce_to(dst, dstb, shift):
            u, ub = C.p32.get()
            ki, kib = C.p32.get()
            ki_i = ki[:, :].bitcast(I32)
            P.add("dve", lambda e: e.tensor_scalar(out=u[:, :T], in0=ang[:, :T], scalar1=float(shift), scalar2=1.0 / TWO_PI, op0=ALU.add, op1=ALU.mult),
                  reads=(angb,), writes=(ub,))
            P.add("dve", lambda e: e.tensor_copy(out=ki_i[:, :T], in_=u[:, :T]), reads=(ub,), writes=(kib,))
            P.add("dve", lambda e: e.tensor_copy(out=u[:, :T], in_=ki_i[:, :T]), reads=(kib,), writes=(ub,))
            P.add("dve", lambda e: e.scalar_tensor_tensor(out=dst[:, :T], in0=u[:, :T], scalar=-C1, in1=ang[:, :T], op0=ALU.mult, op1=ALU.add),
                  reads=(ub, angb), writes=(dstb,))
            P.add("dve", lambda e: e.scalar_tensor_tensor(out=dst[:, :T], in0=u[:, :T], scalar=-C2, in1=dst[:, :T], op0=ALU.mult, op1=ALU.add),
                  reads=(ub, dstb), writes=(dstb,))
            if shift:
                P.add("dve", lambda e: e.tensor_scalar(out=dst[:, :T], in0=dst[:, :T], scalar1=float(shift), scalar2=None, op0=ALU.add),
                      reads=(dstb,), writes=(dstb,))
            P.add("dve", lambda e: e.tensor_scalar(out=u[:, :T], in0=dst[:, :T], scalar1=math.pi, scalar2=-TWO_PI, op0=ALU.is_gt, op1=ALU.mult),
                  reads=(dstb,), writes=(ub,))
            P.add("dve", lambda e: e.tensor_tensor(out=dst[:, :T], in0=dst[:, :T], in1=u[:, :T], op=ALU.add), reads=(dstb, ub), writes=(dstb,))
            P.add("dve", lambda e: e.tensor_scalar(out=u[:, :T], in0=dst[:, :T], scalar1=-math.pi, scalar2=TWO_PI, op0=ALU.is_lt, op1=ALU.mult),
                  reads=(dstb,), writes=(ub,))
            P.add("dve", lambda e: e.tensor_tensor(out=dst[:, :T], in0=dst[:, :T], in1=u[:, :T], op=ALU.add), reads=(dstb, ub), writes=(dstb,))
            P.add("dve", lambda e: e.tensor_scalar(out=dst[:, :T], in0=dst[:, :T], scalar1=math.pi, scalar2=-math.pi, op0=ALU.min, op1=ALU.max),
                  reads=(dstb,), writes=(dstb,))

        (cos_t, cosb), (sin_t, sinb), (cosk, coskb), (sink, sinkb) = C.tabs
        reduce_to(cos_t, cosb, math.pi / 2)
        reduce_to(sin_t, sinb, 0.0)
        P.add("act", lambda e: e.activation(out=cos_t[:, :T], in_=cos_t[:, :T], func=AF.Sin), reads=(cosb,), writes=(cosb,))
        P.add("act", lambda e: e.activation(out=sin_t[:, :T], in_=sin_t[:, :T], func=AF.Sin), reads=(sinb,), writes=(sinb,))
        P.add("dve", lambda e: e.tensor_scalar(out=sin_t[:, :T], in0=sin_t[:, :T], scalar1=C.col(1), scalar2=None, op0=ALU.mult),
              reads=(sinb, C.c32b), writes=(sinb,))
        P.add("dve", lambda e: e.tensor_scalar(out=cosk[:, :T], in0=cos_t[:, :T], scalar1=s_k, scalar2=None, op0=ALU.mult), reads=(cosb,), writes=(coskb,))
        P.add("dve", lambda e: e.tensor_scalar(out=sink[:, :T], in0=sin_t[:, :T], scalar1=s_k, scalar2=None, op0=ALU.mult), reads=(sinb,), writes=(sinkb,))

    return [Step(None, body)]


def phase_mixproj(C, win, winsw, outs, outb, l, t):
    P, T, DC, D = C.P, C.cfg["T"], C.DC, C.cfg["D"]
    tok = slice(t * T, (t + 1) * T)
    steps = []

    def store(seg, ci, tile, tb):
        dst = outs[seg]
        P.add("sp", lambda e: [e.dma_start(out=dst[ci, :, tok], in_=tile[:, :T])], reads=(tb,), writes=(outb(seg, ci, t),), dma=True, key=tb.name)

    def seg_steps(seg, nchunks, epilogue):
        c_base = SEG[seg] if seg != "gb" else SEG["ga"] + D
        nblk = nchunks // 2
        for sg in range((nblk + 1) // 2):
            blks = [b for b in (2 * sg, 2 * sg + 1) if b < nblk]
            loads = [(bi * DC * 256, DC * 256, win[c_base // 256 + b]) for bi, b in enumerate(blks)]

            def body(slot, sbuf, blks=blks):
                for bi, b in enumerate(blks):
                    w = wview(slot, bi * DC * 256, DC, 256)
                    for j in range(2):
                        ci = b * 2 + j
                        pm, pmb = C.nextps()
                        mm_group(C, pm, pmb, w, sbuf, j * 128, C.hT, C.hTb, DC, T)
                        epilogue(ci, pm, pmb)

            steps.append(Step(loads, body))

    def plain_steps(seg, nchunks, func, dt32):
        def epi(ci, pm, pmb):
            st, stb = C.p32.get() if dt32 else C.p16.get()
            P.add("act", lambda e, pm=pm, st=st: e.activation(out=st[:, :T], in_=pm[:, :T], func=func), reads=(pmb,), writes=(stb,))
            store(seg, ci, st, stb)

        seg_steps(seg, nchunks, epi)

    def norm_steps(seg, which):
        def epi(ci, pm, pmb):
            sq, sqb = C.p32.get()
            P.add("act", lambda e, pm=pm, sq=sq: e.activation(out=sq[:, :T], in_=pm[:, :T], func=AF.Square), reads=(pmb,), writes=(sqb,))
            ss, ssb = C.nextps()
            for (lo, hi) in halves(T):
                P.add("pe", lambda e, ss=ss, sq=sq, lo=lo, hi=hi: e.matmul(ss[:, lo:hi], lhsT=C.bd64, rhs=sq[:, lo:hi], start=True, stop=True),
                      reads=(sqb, C.c32b), writes=(ssb,))
            rstd_from_ss(C, ss, ssb, 64, T)
            st, stb = C.p16.get()
            P.add("dve", lambda e, pm=pm, st=st: e.scalar_tensor_tensor(out=st[:, :T], in0=pm[:, :T], scalar=C.gcol2(l, which), in1=C.rstd[:, :T],
                                                                         op0=ALU.mult, op1=ALU.mult),
                  reads=(pmb, C.rstdb, C.gainsb), writes=(stb,))
            store(seg, ci, st, stb)

        seg_steps(seg, 8, epi)

    def rot_steps(seg, swoff, tc, ts):
        NCOL = 256
        for sg in range(1024 // NCOL):
            loads = [(0, DC * NCOL, win[SEG[seg] // 256 + sg]), (DC * NCOL, DC * NCOL, winsw[swoff // 256 + sg])]

            def body(slot, sbuf, sg=sg):
                w = wview(slot, 0, DC, NCOL)
                ws = wview(slot, DC * NCOL, DC, NCOL)
                for j in range(NCOL // 128):
                    ci = sg * (NCOL // 128) + j
                    pm, pmb = C.nextps()
                    pw, pwb = C.nextps()
                    mm_group(C, pm, pmb, w, sbuf, j * 128, C.hT, C.hTb, DC, T)
                    mm_group(C, pw, pwb, ws, sbuf, j * 128, C.hT, C.hTb, DC, T)
                    t1, t1b = C.p32.get()
                    t2, t2b = C.p32.get()
                    P.add("dve", lambda e, pm=pm, t1=t1: e.tensor_tensor(out=t1[:, :T], in0=pm[:, :T], in1=C.tabs[tc][0][:, :T], op=ALU.mult),
                          reads=(pmb, C.tabs[tc][1]), writes=(t1b,))
                    P.add("dve", lambda e, pw=pw, t2=t2: e.tensor_tensor(out=t2[:, :T], in0=pw[:, :T], in1=C.tabs[ts][0][:, :T], op=ALU.mult),
                          reads=(pwb, C.tabs[ts][1]), writes=(t2b,))
                    st, stb = C.p16.get()
                    P.add("dve", lambda e, t1=t1, t2=t2, st=st: e.tensor_tensor(out=st[:, :T], in0=t1[:, :T], in1=t2[:, :T], op=ALU.add),
                          reads=(t1b, t2b), writes=(stb,))
                    store(seg, ci, st, stb)

            steps.append(Step(loads, body))

    norm_steps("qa", 0)
    norm_steps("ka", 1)
    plain_steps("va", 8, AF.Copy, False)
    rot_steps("qr", 0, 0, 1)
    rot_steps("kr", 1024, 2, 3)
    plain_steps("vr", 8, AF.Copy, False)
    plain_steps("gr", 8, AF.Silu, True)
    plain_steps("ga", DC, AF.Sigmoid, True)
    plain_steps("gb", DC, AF.Sigmoid, True)
    return steps


def phase_merge(C, xd, xdb, yAd, yBd, ybuf, gates, gateb, wua, wub, wout, l, t, lam_init):
    P, T, DC = C.P, C.cfg["T"], C.DC
    tok = slice(t * T, (t + 1) * T)
    steps = []

    def pre(_a, _b):
        for which, yd in ((0, yAd), (1, yBd)):
            for h in range(8):
                xs, xsb = C.p32.get()
                P.add("sp", lambda e, xs=xs, yd=yd, h=h: [e.dma_start(out=xs[:, :T], in_=yd[h, :, tok])], reads=(ybuf(which, h, t),), writes=(xsb,),
                      dma=True, key=xsb.name)
                sq, sqb = C.p32.get()
                P.add("act", lambda e, xs=xs, sq=sq: e.activation(out=sq[:, :T], in_=xs[:, :T], func=AF.Square), reads=(xsb,), writes=(sqb,))
                ss, ssb = C.nextps()
                for (lo, hi) in halves(T):
                    P.add("pe", lambda e, ss=ss, sq=sq, lo=lo, hi=hi: e.matmul(ss[:, lo:hi], lhsT=C.ones, rhs=sq[:, lo:hi], start=True, stop=True),
                          reads=(sqb, C.c32b), writes=(ssb,))
                rstd_from_ss(C, ss, ssb, 128, T)
                dst = C.act[:, which * 8 + h, :]
                dstb = C.actb[which * 8 + h]
                if which == 0:
                    P.add("dve", lambda e, xs=xs, dst=dst: e.scalar_tensor_tensor(out=dst, in0=xs[:, :T], scalar=C.gcol2(l, 2), in1=C.rstd[:, :T],
                                                                                   op0=ALU.mult, op1=ALU.mult),
                          reads=(xsb, C.rstdb, C.gainsb), writes=(dstb,))
                else:
                    P.add("dve", lambda e, xs=xs: e.scalar_tensor_tensor(out=xs[:, :T], in0=xs[:, :T], scalar=C.gcol2(l, 3), in1=C.rstd[:, :T],
                                                                          op0=ALU.mult, op1=ALU.mult),
                          reads=(xsb, C.rstdb, C.gainsb), writes=(xsb,))
                    gs, gsb_ = C.p32.get()
                    P.add("sp", lambda e, gs=gs, h=h: [e.dma_start(out=gs[:, :T], in_=gates["gr"][h, :, tok])], reads=(gateb("gr", h, t),), writes=(gsb_,),
                          dma=True, key=gsb_.name)
                    P.add("dve", lambda e, xs=xs, gs=gs, dst=dst: e.tensor_tensor(out=dst, in0=xs[:, :T], in1=gs[:, :T], op=ALU.mult),
                          reads=(xsb, gsb_), writes=(dstb,))

    steps.append(Step(None, pre))
    nblk = DC // 2
    for sg in range((nblk + 1) // 2):
        blks = [b for b in (2 * sg, 2 * sg + 1) if b < nblk]
        loads = [(bi * 2048, 2048, wua[b]) for bi, b in enumerate(blks)] + [(4096 + bi * 2048, 2048, wub[b]) for bi, b in enumerate(blks)]

        def body(slot, sbuf, blks=blks):
            for bi, b in enumerate(blks):
                wa = wview(slot, bi * 2048, 8, 256)
                wb = wview(slot, 4096 + bi * 2048, 8, 256)
                for j in range(2):
                    oc = b * 2 + j
                    pa, pab = C.nextps()
                    pb, pbb = C.nextps()
                    mm_group(C, pa, pab, wa, sbuf, j * 128, C.act[:, 0:8, :], C.actb[0:8], 8, T)
                    mm_group(C, pb, pbb, wb, sbuf, j * 128, C.act[:, 8:16, :], C.actb[8:16], 8, T)
                    ga_, gab = C.p32.get()
                    gb_, gbb = C.p32.get()
                    P.add("sp", lambda e, ga_=ga_, oc=oc: [e.dma_start(out=ga_[:, :T], in_=gates["ga"][oc, :, tok])], reads=(gateb("ga", oc, t),), writes=(gab,),
                          dma=True, key=gab.name)
                    P.add("sp", lambda e, gb_=gb_, oc=oc: [e.dma_start(out=gb_[:, :T], in_=gates["gb"][oc, :, tok])], reads=(gateb("gb", oc, t),), writes=(gbb,),
                          dma=True, key=gbb.name)
                    P.add("dve", lambda e, pa=pa, ga_=ga_: e.scalar_tensor_tensor(out=ga_[:, :T], in0=pa[:, :T], scalar=float(1.0 - lam_init), in1=ga_[:, :T], op0=ALU.mult, op1=ALU.mult), reads=(pab, gab), writes=(gab,))
                    P.add("dve", lambda e, pb=pb, gb_=gb_: e.tensor_tensor(out=gb_[:, :T], in0=pb[:, :T], in1=gb_[:, :T], op=ALU.mult), reads=(pbb, gbb), writes=(gbb,))
                    P.add("dve", lambda e, ga_=ga_, gb_=gb_, oc=oc: e.tensor_tensor(out=C.hT[:, oc, :], in0=ga_[:, :T], in1=gb_[:, :T], op=ALU.add),
                          reads=(gab, gbb), writes=(C.hTb[oc],))

        steps.append(Step(loads, body))
    for sg in range((nblk + 1) // 2):
        blks = [b for b in (2 * sg, 2 * sg + 1) if b < nblk]
        loads = [(bi * DC * 256, DC * 256, wout[b]) for bi, b in enumerate(blks)]

        def body(slot, sbuf, blks=blks):
            for bi, b in enumerate(blks):
                w = wview(slot, bi * DC * 256, DC, 256)
                for j in range(2):
                    oc = b * 2 + j
                    yp, ypb = C.nextps()
                    mm_group(C, yp, ypb, w, sbuf, j * 128, C.hT, C.hTb, DC, T)
                    residual_store(C, xd, xdb, oc, t, yp, ypb, 1.0)

        steps.append(Step(loads, body))
    return steps


def phase_ple(C, xd, xdb, pd, wpg, wpe, t):
    P, T, DC = C.P, C.cfg["T"], C.DC
    tok = slice(t * T, (t + 1) * T)
    steps = []

    def pre(_a, _b):
        P.add("pool", lambda e: [e.dma_start(out=C.pT[:, :, :], in_=pd[:, tok].rearrange("(k p) n -> p k n", p=128))], writes=(C.pTb,), dma=True, key="pT")

    steps.append(Step(None, pre))
    NCOL = 256
    for og in range(DC * 128 // NCOL):
        c0 = og * NCOL
        loads = [(0, DC * NCOL, wpg[og]), (DC * NCOL, 2 * NCOL, wpe[og])]

        def body(slot, sbuf, og=og):
            wg_ = wview(slot, 0, DC, NCOL)
            we_ = wview(slot, DC * NCOL, 2, NCOL)
            for j in range(NCOL // 128):
                oc = og * (NCOL // 128) + j
                pg, pgb = C.nextps()
                pe_, peb = C.nextps()
                mm_group(C, pg, pgb, wg_, sbuf, j * 128, C.hT, C.hTb, DC, T)
                mm_group(C, pe_, peb, we_, sbuf, j * 128, C.pT, [C.pTb, C.pTb], 2, T)
                sg, sgb = C.p32.get()
                P.add("act", lambda e, pg=pg, sg=sg: e.activation(out=sg[:, :T], in_=pg[:, :T], func=AF.Sigmoid), reads=(pgb,), writes=(sgb,))
                P.add("dve", lambda e, pe_=pe_, sg=sg: e.tensor_tensor(out=sg[:, :T], in0=sg[:, :T], in1=pe_[:, :T], op=ALU.mult), reads=(sgb, peb), writes=(sgb,))
                residual_store(C, xd, xdb, oc, t, None, None, 1.0, extra=(sg, sgb))

        steps.append(Step(loads, body))
    return steps


def phase_attn(nc, P, es, cfg, d):
    S = cfg["S"]
    NCH = S // 128
    NQT = S // 512
    sb = lambda name, shape, dt: es.enter_context(nc.sbuf_tensor("s_" + name, shape, dt))
    A = sb("bigA", [128, S], BF16)
    B = sb("bigB", [128, S], BF16)
    Cc = sb("bigC", [128, S], BF16)
    Dd = sb("bigD", [128, NCH, 128], BF16)
    Ab, Bb, Cb, Db = P.buf("bigA"), P.buf("bigB"), P.buf("bigC"), P.buf("bigD")
    C3 = Cc[:, :].rearrange("p (n k) -> p n k", k=128)
    k32 = sb("k32", [128, 128 + 128 + 4], F32)
    k32b = P.buf("k32")
    k16 = sb("k16", [128, 128 + 128 + 896], BF16)
    k16b = P.buf("k16")
    lamt = sb("lamt", [128, 256], F32)
    lamb_ = P.buf("lamt")
    nlam = sb("nlam", [128, 4], F32)
    nlamb = P.buf("nlam")
    ET = Rot(P, es, nc, "ET", 3, [128, 1024], BF16)
    stg = Rot(P, es, nc, "vstg", 2, [128, 512], BF16)
    e32 = Rot(P, es, nc, "e32_", 6, [128, 1024], F32)
    o32 = Rot(P, es, nc, "o32_", 3, [128, 512], F32)
    s16 = Rot(P, es, nc, "s16_", 4, [128, 128], BF16)
    Rf = sb("Rf", [128, 128], F32)
    Rfb = P.buf("Rf")
    Rb = Rot(P, es, nc, "Rb", 2, [128, 128], BF16)
    ps = [(es.enter_context(nc.psum_tensor("aps%d" % i, [128, 1024], F32)), P.buf("aps", i)) for i in range(4)]
    ident = k16[:, 0:128]
    ones16 = k16[:, 128:256]
    maskW = k16[:, 256:256 + 896]
    decT, xiT = k32[:, 0:128], k32[:, 128:256]
    zeta, cd, linit = k32[:, 256:257], k32[:, 257:258], k32[:, 258:259]

    P.add("sp", lambda e: [e.dma_start(out=k32[:], in_=d["k32"])], writes=(k32b,), dma=True, key="k32")
    P.add("sp", lambda e: [e.dma_start(out=k16[:], in_=d["k16"])], writes=(k16b,), dma=True, key="k16")
    P.add("sp", lambda e: [e.dma_start(out=lamt[:], in_=d["lam"])], writes=(lamb_,), dma=True, key="lam")
    P.add("dve", lambda e: e.tensor_tensor(out=lamt[:, 0:64], in0=lamt[:, 0:64], in1=lamt[:, 64:128], op=ALU.mult), reads=(lamb_,), writes=(lamb_,))
    P.add("dve", lambda e: e.tensor_tensor(out=lamt[:, 128:192], in0=lamt[:, 128:192], in1=lamt[:, 192:256], op=ALU.mult), reads=(lamb_,), writes=(lamb_,))
    P.add("dve", lambda e: e.reduce_sum(out=nlam[:, 0:1], in_=lamt[:, 0:64], axis=AX.X), reads=(lamb_,), writes=(nlamb,))
    P.add("dve", lambda e: e.reduce_sum(out=nlam[:, 1:2], in_=lamt[:, 128:192], axis=AX.X), reads=(lamb_,), writes=(nlamb,))
    P.add("act", lambda e: e.activation(out=nlam[:, 0:2], in_=nlam[:, 0:2], func=AF.Exp), reads=(nlamb,), writes=(nlamb,))
    P.add("dve", lambda e: e.tensor_tensor(out=nlam[:, 2:3], in0=nlam[:, 1:2], in1=nlam[:, 0:1], op=ALU.subtract), reads=(nlamb,), writes=(nlamb,))
    P.add("dve", lambda e: e.tensor_tensor(out=nlam[:, 3:4], in0=nlam[:, 2:3], in1=linit, op=ALU.subtract), reads=(nlamb, k32b), writes=(nlamb,))

    def load_big(dst, dstb, src, key, rows=None):
        q = S // 4
        for i in range(4):
            if rows is None:
                P.add("sp", lambda e, i=i: [e.dma_start(out=dst[:, i * q:(i + 1) * q], in_=src[:, i * q:(i + 1) * q])], writes=(dstb,), dma=True, key=key)
            else:
                r0, r1 = rows
                P.add("sp", lambda e, i=i: [e.dma_start(out=dst[r0:r1, i * q:(i + 1) * q], in_=src[r0:r1, i * q:(i + 1) * q])], writes=(dstb,), dma=True, key=key)

    def transpose_into(dst3, dstb, src_dram=None, src_sb=None, src_sbb=None, scale_col=None):
        for g in range(NCH // 4):
            if src_dram is not None:
                st, stb = stg.get()
                P.add("sp", lambda e, st=st, g=g: [e.dma_start(out=st[:, :], in_=src_dram[:, g * 512:(g + 1) * 512])], writes=(stb,), dma=True, key=stb.name)
                srcv, srcb = st, stb
                off = 0
            else:
                srcv, srcb = src_sb, src_sbb
                off = g * 512
            pt, ptb = ps[g % 2]
            ptv = pt[:, :].bitcast(BF16)
            for j in range(4):
                P.add("pe", lambda e, j=j, srcv=srcv, off=off, ptv=ptv: e.transpose(out=ptv[:, j * 128:(j + 1) * 128], in_=srcv[:, off + j * 128:off + (j + 1) * 128], identity=ident),
                      reads=(srcb, k16b), writes=(ptb,))
            dv = dst3[:, g * 4:(g + 1) * 4, :]
            sv = ptv[:, 0:512].rearrange("p (n k) -> p n k", k=128)
            if scale_col is None:
                P.add("act", lambda e, dv=dv, sv=sv: e.activation(out=dv, in_=sv, func=AF.Copy), reads=(ptb,), writes=(dstb,))
            else:
                P.add("dve", lambda e, dv=dv, sv=sv: e.tensor_scalar(out=dv, in0=sv, scalar1=scale_col, scalar2=None, op0=ALU.mult), reads=(ptb, k32b), writes=(dstb,))

    load_big(A, Ab, d["qa"], "bigA")
    P.add("pool", lambda e: e.memset(B[64:128, :], 0.0), writes=(Bb,))
    P.add("pool", lambda e: e.memset(Cc[0:64, :], 0.0), writes=(Cb,))
    load_big(B, Bb, d["ka"], "bigB", rows=(0, 64))
    load_big(Cc, Cb, d["ka"], "bigC", rows=(64, 128))
    transpose_into(Dd, Db, src_dram=d["va"])

    items = [(qt, kt) for qt in range(NQT) for kt in range(4 * (qt + 1))]
    Ops, Opb = ps[2]
    Lps, Lpb = ps[3]

    def emit_qk(i):
        qt, kt = items[i]
        Sp, Spb = ps[i % 2]
        P.add("pe", lambda e: e.matmul(Sp[:, 0:512], lhsT=B[:, kt * 128:(kt + 1) * 128], rhs=A[:, qt * 512:(qt + 1) * 512], start=True, stop=True),
              reads=(Ab, Bb), writes=(Spb,))
        P.add("pe", lambda e: e.matmul(Sp[:, 512:1024], lhsT=Cc[:, kt * 128:(kt + 1) * 128], rhs=A[:, qt * 512:(qt + 1) * 512], start=True, stop=True),
              reads=(Ab, Cb), writes=(Spb,))

    emit_qk(0)
    for i, (qt, kt) in enumerate(items):
        if i + 1 < len(items):
            emit_qk(i + 1)
        Sp, Spb = ps[i % 2]
        E, Eb = ET.get()
        P.add("act", lambda e, Sp=Sp, E=E: e.activation(out=E[:, :], in_=Sp[:, :], func=AF.Exp, scale=0.125), reads=(Spb,), writes=(Eb,))
        nk = 4 * (qt + 1)
        if kt >= 4 * qt:
            j = kt - 4 * qt
            m0 = 384 - 128 * j
            for c in range(2):
                P.add("dve", lambda e, E=E, c=c, m0=m0: e.tensor_tensor(out=E[:, c * 512:(c + 1) * 512], in0=E[:, c * 512:(c + 1) * 512], in1=maskW[:, m0:m0 + 512], op=ALU.mult),
                      reads=(Eb, k16b), writes=(Eb,))
        for c in range(2):
            P.add("pe", lambda e, E=E, c=c, kt=kt, nk=nk: e.matmul(Ops[:, c * 512:(c + 1) * 512], lhsT=Dd[:, kt, :], rhs=E[:, c * 512:(c + 1) * 512], start=(kt == 0), stop=(kt == nk - 1)),
                  reads=(Eb, Db), writes=(Opb,))
        for c in range(2):
            P.add("pe", lambda e, E=E, c=c, kt=kt, nk=nk: e.matmul(Lps[:, c * 512:(c + 1) * 512], lhsT=ones16, rhs=E[:, c * 512:(c + 1) * 512], start=(kt == 0), stop=(kt == nk - 1)),
                  reads=(Eb, k16b), writes=(Lpb,))
        if kt == nk - 1:
            Lc, Lcb = e32.get()
            Oc, Ocb = e32.get()
            P.add("act", lambda e, Lc=Lc: e.activation(out=Lc[:, :], in_=Lps[:, :], func=AF.Copy), reads=(Lpb,), writes=(Lcb,))
            P.add("dve", lambda e, Oc=Oc: e.tensor_copy(out=Oc[:, :], in_=Ops[:, :]), reads=(Opb,), writes=(Ocb,))
            P.add("dve", lambda e, Lc=Lc: e.reciprocal(out=Lc[:, :], in_=Lc[:, :]), reads=(Lcb,), writes=(Lcb,))
            P.add("dve", lambda e, Lc=Lc, Oc=Oc: e.tensor_tensor(out=Oc[:, :], in0=Oc[:, :], in1=Lc[:, :], op=ALU.mult), reads=(Lcb, Ocb), writes=(Ocb,))
            yo, yob = o32.get()
            P.add("dve", lambda e, Oc=Oc, yo=yo: e.scalar_tensor_tensor(out=yo[:, :], in0=Oc[:, 512:1024], scalar=nlam[:, 3:4], in1=Oc[:, 0:512], op0=ALU.mult, op1=ALU.add),
                  reads=(Ocb, nlamb), writes=(yob,))
            P.add("sp", lambda e, yo=yo, qt=qt: [e.dma_start(out=d["yA"][:, qt * 512:(qt + 1) * 512], in_=yo[:, :])], reads=(yob,), writes=(P.buf("yAout"),), dma=True, key=yob.name)

    load_big(A, Ab, d["qr"], "bigA")
    load_big(B, Bb, d["kr"], "bigB")
    transpose_into(Dd, Db, src_dram=d["vr"])
    transpose_into(C3, Cb, src_sb=B, src_sbb=Bb, scale_col=zeta)
    qbufs = [[P.buf("rq", i, q) for q in range(4)] for i in range(3)]
    yo = yob = None
    for n in range(NCH):
        q4 = n % 4
        cs = slice(n * 128, (n + 1) * 128)
        scp, scb = ps[0][0][:, q4 * 128:(q4 + 1) * 128], qbufs[0][q4]
        op_, opb = ps[1][0][:, q4 * 128:(q4 + 1) * 128], qbufs[1][q4]
        kvp, kvb = ps[2][0][:, q4 * 128:(q4 + 1) * 128], qbufs[2][q4]
        if n < 4:
            scb_w = (scb, ps[0][1])
            opb_w = (opb, ps[1][1])
            kvb_w = (kvb, ps[2][1])
        else:
            scb_w, opb_w, kvb_w = (scb,), (opb,), (kvb,)
        P.add("pe", lambda e, scp=scp, cs=cs: e.matmul(scp, lhsT=B[:, cs], rhs=A[:, cs], start=True, stop=True), reads=(Ab, Bb), writes=scb_w)
        sT, sTb = s16.get()
        P.add("dve", lambda e, scp=scp, sT=sT: e.tensor_tensor(out=sT[:, :], in0=scp, in1=decT, op=ALU.mult), reads=(scb, k32b), writes=(sTb,))
        if n > 0:
            qx, qxb = s16.get()
            P.add("dve", lambda e, qx=qx, cs=cs: e.tensor_tensor(out=qx[:, :], in0=A[:, cs], in1=xiT, op=ALU.mult), reads=(Ab, k32b), writes=(qxb,))
        P.add("pe", lambda e, op_=op_, sT=sT, n=n: e.matmul(op_, lhsT=Dd[:, n, :], rhs=sT[:, :], start=True, stop=(n == 0)), reads=(Db, sTb), writes=opb_w)
        if n > 0:
            rb, rbb = Rb.items[(n - 1) % 2]
            P.add("pe", lambda e, op_=op_, qx=qx, rb=rb: e.matmul(op_, lhsT=rb[:, :], rhs=qx[:, :], start=False, stop=True), reads=(rbb, qxb), writes=(opb,))
        if q4 == 0:
            yo, yob = o32.get()
        P.add("act", lambda e, op_=op_, yo=yo, q4=q4: e.activation(out=yo[:, q4 * 128:(q4 + 1) * 128], in_=op_, func=AF.Copy), reads=(opb,), writes=(yob,))
        if q4 == 3:
            n0 = n - 3
            P.add("sp", lambda e, yo=yo, n0=n0: [e.dma_start(out=d["yB"][:, n0 * 128:(n0 + 4) * 128], in_=yo[:, :])], reads=(yob,), writes=(P.buf("yBout"),), dma=True, key=yob.name)
        if n < NCH - 1:
            P.add("pe", lambda e, kvp=kvp, n=n: e.matmul(kvp, lhsT=C3[:, n, :], rhs=Dd[:, n, :], start=True, stop=True), reads=(Cb, Db), writes=kvb_w)
            if n == 0:
                P.add("dve", lambda e, kvp=kvp: e.tensor_copy(out=Rf[:, :], in_=kvp), reads=(kvb,), writes=(Rfb,))
            else:
                P.add("dve", lambda e, kvp=kvp: e.scalar_tensor_tensor(out=Rf[:, :], in0=Rf[:, :], scalar=cd, in1=kvp, op0=ALU.mult, op1=ALU.add),
                      reads=(kvb, Rfb, k32b), writes=(Rfb,))
            rb, rbb = Rb.items[n % 2]
            P.add("act", lambda e, rb=rb: e.activation(out=rb[:, :], in_=Rf[:, :], func=AF.Copy), reads=(Rfb,), writes=(rbb,))


def dram_in(nc, name, shape, dt):
    return nc.dram_tensor(name, list(shape), dt, kind="ExternalInput").ap()


def dram_out(nc, name, shape, dt):
    return nc.dram_tensor(name, list(shape), dt, kind="ExternalOutput").ap()


def declare_front_weights(nc, cfg, sfx):
    D, F = cfg["D"], cfg["F"]
    DC, FH = D // 128, F // 256
    return dict(wg=dram_in(nc, "f1g" + sfx, [F // 256, 128, DC * 256], F32), wu=dram_in(nc, "f1u" + sfx, [F // 256, 128, DC * 256], F32),
                wd=dram_in(nc, "f1d" + sfx, [2, DC, 128, FH * 128], F32),
                win=dram_in(nc, "win" + sfx, [(7168 + 2 * D) // 256, 128, DC * 256], F32), winsw=dram_in(nc, "winsw" + sfx, [8, 128, DC * 256], F32))


def declare_front_outs(nc, cfg, sfx):
    DC, TOK = cfg["D"] // 128, cfg["S"] // NCORE
    o = {}
    for s in ("qa", "ka", "va", "qr", "kr", "vr"):
        o[s] = dram_out(nc, s + sfx, [8, 128, TOK], BF16)
    o["gr"] = dram_out(nc, "gr" + sfx, [8, 128, TOK], F32)
    o["ga"] = dram_out(nc, "ga" + sfx, [DC, 128, TOK], F32)
    o["gb"] = dram_out(nc, "gb" + sfx, [DC, 128, TOK], F32)
    return o


def emit_front(C, xd, xdb, w, outs, outb, posd, l, t):
    steps = []
    steps += phase_norm(C, xd, xdb, l, 0, t)
    steps += phase_ffn(C, xd, xdb, w["wg"], w["wu"], w["wd"], t)
    steps += phase_norm(C, xd, xdb, l, 1, t)
    steps += rotary_tables(C, posd, t)
    steps += phase_mixproj(C, w["win"], w["winsw"], outs, outb, l, t)
    return steps


def build_A(cfg):
    nc = bass.Bass("TRN2", target_bir_lowering=False)
    DC, TOK, T = cfg["D"] // 128, cfg["S"] // NCORE, cfg["T"]
    with ExitStack() as es:
        P = Prog(nc)
        xin = dram_in(nc, "xin", [DC, 128, TOK], F32)
        xd = dram_out(nc, "xout", [DC, 128, TOK], F32)
        posd = dram_in(nc, "pos", [1, TOK], I32)
        c32_d = dram_in(nc, "c32", [128, 264], F32)
        gains_d = dram_in(nc, "gains", [128, cfg["L"] * (4 * DC + 4)], F32)
        w = declare_front_weights(nc, cfg, "")
        outs = declare_front_outs(nc, cfg, "")
        C = TokCtx(nc, P, es, cfg)
        C.load_consts(c32_d, gains_d)
        xdb = lambda dc, t: P.buf("x", dc, t)
        outb = lambda seg, ci, t: P.buf("o", seg, ci, t)
        for dc in range(DC):
            for t in range(TOK // T):
                xs, xsb = C.p32.get()
                tok = slice(t * T, (t + 1) * T)
                P.add("sp", lambda e, xs=xs, dc=dc, tok=tok: [e.dma_start(out=xs[:, :T], in_=xin[dc, :, tok])], writes=(xsb,), dma=True, key=xsb.name)
                P.add("sp", lambda e, xs=xs, dc=dc, tok=tok: [e.dma_start(out=xd[dc, :, tok], in_=xs[:, :T])], reads=(xsb,), writes=(xdb(dc, t),), dma=True, key=xsb.name)
        steps = []
        for t in range(TOK // T):
            steps += emit_front(C, xd, xdb, w, outs, outb, posd, 0, t)
        run_steps(P, steps, C.slots)
        P.emit(es)
    return nc


def build_B(cfg):
    nc = bass.Bass("TRN2", target_bir_lowering=False)
    S = cfg["S"]
    with ExitStack() as es:
        P = Prog(nc)
        d = {}
        for s in ("qa", "ka", "va", "qr", "kr", "vr"):
            d[s] = dram_in(nc, s, [128, S], BF16)
        d["k32"] = dram_in(nc, "k32", [128, 260], F32)
        d["k16"] = dram_in(nc, "k16", [128, 1152], BF16)
        d["lam"] = dram_in(nc, "lam", [128, 256], F32)
        d["yA"] = dram_out(nc, "yA", [128, S], F32)
        d["yB"] = dram_out(nc, "yB", [128, S], F32)
        phase_attn(nc, P, es, cfg, d)
        P.emit(es)
    return nc


def build_C(cfg, l, with_front):
    nc = bass.Bass("TRN2", target_bir_lowering=False)
    D, F = cfg["D"], cfg["F"]
    DC, TOK, T = D // 128, cfg["S"] // NCORE, cfg["T"]
    lam_init = 0.8 - 0.6 * math.exp(-0.3 * l)
    with ExitStack() as es:
        P = Prog(nc)
        xin = dram_in(nc, "xin", [DC, 128, TOK], F32)
        xd = dram_out(nc, "xout", [DC, 128, TOK], F32)
        c32_d = dram_in(nc, "c32", [128, 264], F32)
        gains_d = dram_in(nc, "gains", [128, cfg["L"] * (4 * DC + 4)], F32)
        yAd = dram_in(nc, "yA", [8, 128, TOK], F32)
        yBd = dram_in(nc, "yB", [8, 128, TOK], F32)
        gates = dict(gr=dram_in(nc, "gr_i", [8, 128, TOK], F32), ga=dram_in(nc, "ga_i", [DC, 128, TOK], F32), gb=dram_in(nc, "gb_i", [DC, 128, TOK], F32))
        pd = dram_in(nc, "pT", [256, TOK], F32)
        FH = F // 256
        wua = dram_in(nc, "wua", [D // 256, 128, 2048], F32)
        wub = dram_in(nc, "wub", [D // 256, 128, 2048], F32)
        wout = dram_in(nc, "wout", [D // 256, 128, DC * 256], F32)
        f2 = dict(wg=dram_in(nc, "f2g", [F // 256, 128, DC * 256], F32), wu=dram_in(nc, "f2u", [F // 256, 128, DC * 256], F32),
                  wd=dram_in(nc, "f2d", [2, DC, 128, FH * 128], F32))
        wpg = dram_in(nc, "wpg", [D // 256, 128, DC * 256], F32)
        wpe = dram_in(nc, "wpe", [D // 256, 128, 512], F32)
        if with_front:
            posd = dram_in(nc, "pos", [1, TOK], I32)
            w = declare_front_weights(nc, cfg, "")
            outs = declare_front_outs(nc, cfg, "")
        C = TokCtx(nc, P, es, cfg)
        C.load_consts(c32_d, gains_d)
        xdb = lambda dc, t: P.buf("x", dc, t)
        outb = lambda seg, ci, t: P.buf("o", seg, ci, t)
        ybuf = lambda which, h, t: P.buf("y", which, h, t)
        gateb = lambda seg, ci, t: P.buf("g", seg, ci, t)
        for dc in range(DC):
            for t in range(TOK // T):
                xs, xsb = C.p32.get()
                tok = slice(t * T, (t + 1) * T)
                P.add("sp", lambda e, xs=xs, dc=dc, tok=tok: [e.dma_start(out=xs[:, :T], in_=xin[dc, :, tok])], writes=(xsb,), dma=True, key=xsb.name)
                P.add("sp", lambda e, xs=xs, dc=dc, tok=tok: [e.dma_start(out=xd[dc, :, tok], in_=xs[:, :T])], reads=(xsb,), writes=(xdb(dc, t),), dma=True, key=xsb.name)
        steps = []
        for t in range(TOK // T):
            skip = cfg.get("skip", ())
            if "merge" not in skip:
                steps += phase_merge(C, xd, xdb, yAd, yBd, ybuf, gates, gateb, wua, wub, wout, l, t, lam_init)
            if "ffn2" not in skip:
                steps += phase_norm(C, xd, xdb, l, 2, t)
                steps += phase_ffn(C, xd, xdb, f2["wg"], f2["wu"], f2["wd"], t)
            if "ple" not in skip:
                steps += phase_norm(C, xd, xdb, l, 3, t)
                steps += phase_ple(C, xd, xdb, pd, wpg, wpe, t)
            if with_front:
                steps += emit_front(C, xd, xdb, w, outs, outb, posd, l + 1, t)
        run_steps(P, steps, C.slots)
        P.emit(es)
    return nc


def make_consts(cfg):
    c32 = np.zeros((128, 264), np.float32)
    c32[:, 0:128] = 1.0
    c32[0:64, 128:192] = 1.0
    c32[64:128, 192:256] = 1.0
    i = np.arange(128) % 64
    c32[:, 256] = (10000.0 ** (-(2.0 * i.astype(np.float32)) / np.float32(128.0))).astype(np.float32)
    c32[0:64, 257] = -1.0
    c32[64:128, 257] = 1.0
    c32[:, 258] = EPS
    return c32


def make_gains(cfg, inp):
    L, DC = cfg["L"], cfg["D"] // 128
    g = np.zeros((128, L * (4 * DC + 4)), np.float32)
    for l in range(L):
        o = l * (4 * DC + 4)
        for w, name in enumerate(("ffn1_norm", "mix_norm", "ffn2_norm", "ple_norm")):
            g[:, o + w * DC:o + (w + 1) * DC] = np.asarray(inp[name][l], np.float32).reshape(DC, 128).T
        g[:, o + 4 * DC + 0] = np.tile(np.asarray(inp["da_q_norm"][l], np.float32), 2)
        g[:, o + 4 * DC + 1] = np.tile(np.asarray(inp["da_k_norm"][l], np.float32), 2)
        g[:, o + 4 * DC + 2] = np.asarray(inp["da_sub_norm"][l], np.float32)
        g[:, o + 4 * DC + 3] = np.asarray(inp["ret_sub_norm"][l], np.float32)
    return g


def attn_consts(h, lam_init):
    k32 = np.zeros((128, 260), np.float32)
    log_g = np.log(np.float32(1.0) - np.float32(2.0) ** (np.float32(-5.0) - np.float32(h))).astype(np.float32)
    idx = np.arange(128, dtype=np.float32)
    rel = idx[None, :] - idx[:, None]
    k32[:, 0:128] = np.where(rel >= 0, np.exp(log_g * np.maximum(rel, 0.0)), 0.0)
    k32[:, 128:256] = np.exp(log_g * (idx + 1.0))[None, :]
    k32[:, 256] = np.exp(log_g * (127.0 - idx))
    k32[:, 257] = np.exp(log_g * 128.0)
    k32[:, 258] = lam_init
    k32[:, 259] = EPS
    k16 = np.zeros((128, 1152), np.float32)
    k16[:, 0:128] = np.eye(128)
    k16[:, 128:256] = 1.0
    u = np.arange(896)[None, :] - 384
    k16[:, 256:] = (u >= np.arange(128)[:, None]).astype(np.float32)
    return k32, k16.astype(ml_dtypes.bfloat16)


def swap_cols(w):
    D = w.shape[0]
    seg = np.concatenate([w[:, SEG["qr"]:SEG["qr"] + 1024], w[:, SEG["kr"]:SEG["kr"] + 1024]], axis=1)
    seg = seg.reshape(D, 16, 2, 64)[:, :, ::-1, :]
    return np.ascontiguousarray(seg.reshape(D, 2048))


def blk(W, ncol=256):
    W = np.asarray(W, np.float32)
    K, N = W.shape
    return np.ascontiguousarray(W.reshape(K // 128, 128, N // ncol, ncol).transpose(2, 1, 0, 3).reshape(N // ncol, 128, (K // 128) * ncol))


def blk_down(Wd):
    F = Wd.shape[0]
    FHr = F // 2
    return np.ascontiguousarray(np.stack([blk(Wd[h * FHr:(h + 1) * FHr, :], 128) for h in range(2)]))


_PROG_CACHE = {}
DEBUG = None


def get_prog(key, builder):
    if key not in _PROG_CACHE:
        _PROG_CACHE[key] = builder()
    return _PROG_CACHE[key]


def run_module(cfg, inp):
    D, F, S, L = cfg["D"], cfg["F"], cfg["S"], cfg["L"]
    DC, TOK = D // 128, S // NCORE
    cores = list(range(NCORE))
    f32 = lambda a: np.ascontiguousarray(np.asarray(a, np.float32))
    x = f32(inp["x"])[0]
    pos = np.ascontiguousarray(np.asarray(inp["positions"], np.int32))
    c32 = make_consts(cfg)
    gains = make_gains(cfg, inp)
    xT = [np.ascontiguousarray(x[c * TOK:(c + 1) * TOK, :].T.reshape(DC, 128, TOK)) for c in cores]
    pTs = [[np.ascontiguousarray(f32(inp["p"][l])[0][c * TOK:(c + 1) * TOK, :].T) for c in cores] for l in range(L)]
    poss = [np.ascontiguousarray(pos[:, c * TOK:(c + 1) * TOK]) for c in cores]

    def front_w(l):
        win = f32(inp["w_in"][l])
        return {"f1g": blk(inp["ffn1_w_gate"][l]), "f1u": blk(inp["ffn1_w_up"][l]), "f1d": blk_down(f32(inp["ffn1_w_down"][l])), "win": blk(win), "winsw": blk(swap_cols(win))}

    ncA = get_prog(("A", str(sorted(cfg.items()))), lambda: build_A(cfg))
    fw = front_w(0)
    maps = [dict(xin=xT[c], pos=poss[c], c32=c32, gains=gains, **fw) for c in cores]
    res = run_bass_kernel_spmd(ncA, maps, core_ids=cores).results
    if DEBUG is not None:
        DEBUG['A'] = res
    for l in range(L):
        xT = [r["xout"] for r in res]
        lam_init = 0.8 - 0.6 * math.exp(-0.3 * l)
        ncB = get_prog(("B", str(sorted(cfg.items()))), lambda: build_B(cfg))
        lam = np.concatenate([np.asarray(inp[n][l], np.float32) for n in ("da_lambda_q1", "da_lambda_k1", "da_lambda_q2", "da_lambda_k2")])
        lam = np.ascontiguousarray(np.broadcast_to(lam[None, :], (128, 256)))
        mapsB = []
        for h in cores:
            k32, k16 = attn_consts(h, lam_init)
            m = dict(k32=k32, k16=k16, lam=lam)
            for s in ("qa", "ka", "va", "qr", "kr", "vr"):
                m[s] = np.ascontiguousarray(np.concatenate([res[c][s][h] for c in cores], axis=1))
            mapsB.append(m)
        resB = run_bass_kernel_spmd(ncB, mapsB, core_ids=cores).results
        if DEBUG is not None:
            DEBUG['B%d' % l] = resB
        wf = l + 1 < L
        ncC = get_prog(("C", l, wf, str(sorted(cfg.items()))), lambda: build_C(cfg, l, wf))
        wts = dict(wua=blk(inp["w_up_a"][l]), wub=blk(inp["w_up_b"][l]), wout=blk(inp["w_out"][l]), f2g=blk(inp["ffn2_w_gate"][l]),
                   f2u=blk(inp["ffn2_w_up"][l]), f2d=blk_down(f32(inp["ffn2_w_down"][l])), wpg=blk(inp["w_ple_gate"][l]), wpe=blk(inp["w_ple_proj"][l]))
        if wf:
            wts.update(front_w(l + 1))
        mapsC = []
        for c in cores:
            m = dict(xin=xT[c], c32=c32, gains=gains, pT=pTs[l][c], gr_i=res[c]["gr"], ga_i=res[c]["ga"], gb_i=res[c]["gb"], **wts)
            m["yA"] = np.ascontiguousarray(np.stack([resB[h]["yA"][:, c * TOK:(c + 1) * TOK] for h in cores]))
            m["yB"] = np.ascontiguousarray(np.stack([resB[h]["yB"][:, c * TOK:(c + 1) * TOK] for h in cores]))
            if wf:
                m["pos"] = poss[c]
            mapsC.append(m)
        res = run_bass_kernel_spmd(ncC, mapsC, core_ids=cores).results
        if DEBUG is not None:
            DEBUG['C%d' % l] = res
    out = np.concatenate([r["xout"].reshape(D, TOK).T for r in res], axis=0)
    return np.ascontiguousarray(out[None].astype(np.float32))


CFG_FULL = dict(D=2048, F=5632, S=16384, T=1024, L=2)


def kernel(**inputs):
    return run_module(CFG_FULL, inputs)
```
